# Optimizing a Trainium2 kernel written in Bass

```python
import math
import jax, jax.numpy as jnp
from jax import lax
import numpy as np

D_MODEL = 4096
BATCH = 1
SEQ = 8192
DEPTH = 4

N_MIXERS = 2
N_MLSTM_LAYERS = (DEPTH + 1) // 2
N_NSA_LAYERS = DEPTH // 2

ML_HEADS = 8
ML_QK_DIM = D_MODEL // 2 // ML_HEADS
ML_V_DIM = D_MODEL // ML_HEADS
ML_CHUNK = 64
GATE_SOFTCAP = 15.0
ML_IN_COLS = 2 * ML_HEADS * ML_QK_DIM + 2 * D_MODEL + 2 * ML_HEADS

NSA_HEAD_DIM = 128
NSA_HEADS = D_MODEL // NSA_HEAD_DIM
NSA_KV_GROUPS = 4
NSA_HPG = NSA_HEADS // NSA_KV_GROUPS
CMP_BLOCK = 32
CMP_STRIDE = 16
CMP_HIDDEN = 2 * NSA_HEAD_DIM
SEL_BLOCK = 64
SEL_TOPK = 16
WINDOW = 512
NSA_Q_BLOCK = 64
NSA_IN_COLS = NSA_HEADS * NSA_HEAD_DIM + 6 * NSA_KV_GROUPS * NSA_HEAD_DIM + 3 * NSA_HEADS

D_FF = 10944
CONV_WIDTH = 3

PLE_DIM = 256

EPS = 1e-6
NEG = -1e9
FORCE = 1e9

kernel_name = "hybrid_mlstm_nsa_alibi_sandwich"


def rms_norm(x, g):
    x32 = x.astype(jnp.float32)
    y = x32 * lax.rsqrt(jnp.mean(x32 * x32, axis=-1, keepdims=True) + EPS) * g.astype(jnp.float32)
    return y.astype(x.dtype)


def softcap(x):
    return GATE_SOFTCAP * jnp.tanh(x / GATE_SOFTCAP)


def masked_softmax(s, mask, axis=-1):
    s = jnp.where(mask, s.astype(jnp.float32), NEG)
    p = jax.nn.softmax(s, axis=axis)
    return jnp.where(mask, p, 0.0)


def mlstm_mixer(h, w_in, b_if, head_norm, w_out):
    B, S, _ = h.shape
    H, Dk, Dv, L = ML_HEADS, ML_QK_DIM, ML_V_DIM, ML_CHUNK
    NC = S // L
    proj = (h @ w_in).astype(jnp.float32)
    o0 = 0
    q = proj[..., o0:o0 + H * Dk].reshape(B, S, H, Dk) * (Dk ** -0.5); o0 += H * Dk
    k = proj[..., o0:o0 + H * Dk].reshape(B, S, H, Dk); o0 += H * Dk
    v = proj[..., o0:o0 + H * Dv].reshape(B, S, H, Dv); o0 += H * Dv
    og = proj[..., o0:o0 + D_MODEL]; o0 += D_MODEL
    ig = softcap(proj[..., o0:o0 + H] + b_if[:H].astype(jnp.float32)); o0 += H
    fg = softcap(proj[..., o0:o0 + H] + b_if[H:].astype(jnp.float32))
    logf = jax.nn.log_sigmoid(fg)

    def to_chunks(t):
        return t.reshape(B, NC, L, H, -1).transpose(1, 0, 3, 2, 4)

    def gate_chunks(t):
        return t.reshape(B, NC, L, H).transpose(1, 0, 3, 2)

    causal = jnp.tril(jnp.ones((L, L), dtype=bool))

    def body(carry, xs):
        C, n, m = carry
        qc, kc, vc, ic, fc = xs
        b = jnp.cumsum(fc, axis=-1)
        dmat = jnp.where(causal, b[..., :, None] - b[..., None, :] + ic[..., None, :], -jnp.inf)
        inter = b + m[..., None]
        m_t = jnp.maximum(inter, jnp.max(dmat, axis=-1))
        s = jnp.einsum('bhtd,bhsd->bhts', qc, kc) * jnp.exp(dmat - m_t[..., None])
        w_inter = jnp.exp(inter - m_t)
        num = w_inter[..., None] * jnp.einsum('bhtd,bhde->bhte', qc, C) + jnp.einsum('bhts,bhse->bhte', s, vc)
        den = w_inter * jnp.einsum('bhtd,bhd->bht', qc, n) + jnp.sum(s, axis=-1)
        hc = num / jnp.maximum(jnp.abs(den), jnp.exp(-m_t))[..., None]
        bL = b[..., -1]
        a = bL[..., None] - b + ic
        m_new = jnp.maximum(bL + m, jnp.max(a, axis=-1))
        wk = jnp.exp(a - m_new[..., None])
        decay = jnp.exp(bL + m - m_new)
        C_new = decay[..., None, None] * C + jnp.einsum('bhs,bhsd,bhse->bhde', wk, kc, vc)
        n_new = decay[..., None] * n + jnp.einsum('bhs,bhsd->bhd', wk, kc)
        return (C_new, n_new, m_new), hc

    init = (jnp.zeros((B, H, Dk, Dv), jnp.float32), jnp.zeros((B, H, Dk), jnp.float32),
            jnp.zeros((B, H), jnp.float32))
    _, hs = lax.scan(body, init, (to_chunks(q), to_chunks(k), to_chunks(v), gate_chunks(ig), gate_chunks(logf)))
    hs = hs.transpose(1, 0, 3, 2, 4).reshape(B, S, H, Dv)
    hs = hs * lax.rsqrt(jnp.mean(hs * hs, axis=-1, keepdims=True) + EPS) * head_norm.astype(jnp.float32).reshape(H, Dv)
    out = hs.reshape(B, S, D_MODEL) * jax.nn.sigmoid(og)
    return out.astype(h.dtype) @ w_out


def nsa_mixer(h, w_in, cmp_pe, cmp_w1, cmp_w2, w_out):
    B, S, _ = h.shape
    H, G, HPG, Dh, QB = NSA_HEADS, NSA_KV_GROUPS, NSA_HPG, NSA_HEAD_DIM, NSA_Q_BLOCK
    f32 = jnp.float32
    proj = (h @ w_in).astype(f32)
    q = proj[..., :H * Dh].reshape(B, S, G, HPG, Dh) * (Dh ** -0.5)
    kv = proj[..., H * Dh:H * Dh + 6 * G * Dh].reshape(B, S, 6, G, Dh)
    k_cmp_raw, v_cmp_raw = kv[:, :, 0], kv[:, :, 1]
    k_slc, v_slc = kv[:, :, 2], kv[:, :, 3]
    k_win, v_win = kv[:, :, 4], kv[:, :, 5]
    gates = jax.nn.sigmoid(proj[..., H * Dh + 6 * G * Dh:]).reshape(B, S, G, HPG, 3)
    slopes = jnp.exp2(-8.0 * jnp.arange(1, H + 1, dtype=f32) / H).reshape(G, HPG)

    n_cmp = (S - CMP_BLOCK) // CMP_STRIDE + 1
    blk_idx = np.arange(n_cmp)[:, None] * CMP_STRIDE + np.arange(CMP_BLOCK)[None, :]

    def compress(raw, j):
        blocks = raw[:, blk_idx] + cmp_pe[j].astype(f32)[None, None, :, None, :]
        hid = jax.nn.gelu(jnp.einsum('bnlgd,lde->bnge', blocks, cmp_w1[j].astype(f32)))
        return jnp.einsum('bnge,ed->bngd', hid, cmp_w2[j].astype(f32))

    k_cmp = compress(k_cmp_raw, 0)
    v_cmp = compress(v_cmp_raw, 1)
    cmp_end = jnp.arange(n_cmp) * CMP_STRIDE + CMP_BLOCK - 1

    n_sel = S // SEL_BLOCK
    k_sel_blocks = k_slc.reshape(B, n_sel, SEL_BLOCK, G, Dh).transpose(0, 3, 1, 2, 4)
    v_sel_blocks = v_slc.reshape(B, n_sel, SEL_BLOCK, G, Dh).transpose(0, 3, 1, 2, 4)
    top_n = min(SEL_TOPK, n_sel)
    ratio = SEL_BLOCK // CMP_STRIDE
    n_off = CMP_BLOCK // CMP_STRIDE
    pool_w = jnp.asarray(np.convolve(np.ones(ratio), np.ones(n_off)), dtype=f32)
    pool_idx = ratio * np.arange(n_sel)[:, None] + np.arange(ratio + n_off - 1)[None, :]
    pool_pad = int(pool_idx.max()) + 1 - n_cmp
    bi = jnp.arange(B)[:, None, None, None]
    gi = jnp.arange(G)[None, None, :, None]
    blk = jnp.arange(n_sel)

    k_win_pad = jnp.pad(k_win, ((0, 0), (WINDOW, 0), (0, 0), (0, 0)))
    v_win_pad = jnp.pad(v_win, ((0, 0), (WINDOW, 0), (0, 0), (0, 0)))

    def block_fn(i):
        start = i * QB
        t = start + jnp.arange(QB)
        qb = lax.dynamic_slice_in_dim(q, start, QB, axis=1)
        gb = lax.dynamic_slice_in_dim(gates, start, QB, axis=1)
        dist_c = t[:, None] - cmp_end[None, :]
        s_c = jnp.einsum('bqghd,bngd->bqghn', qb, k_cmp) - slopes[None, None, :, :, None] * dist_c[None, :, None, None, :]
        p_c = masked_softmax(s_c, (dist_c >= 0)[None, :, None, None, :])
        o_c = jnp.einsum('bqghn,bngd->bqghd', p_c, v_cmp)
        imp = jnp.pad(p_c.sum(axis=3), ((0, 0), (0, 0), (0, 0), (0, pool_pad)))
        imp_sel = jnp.einsum('bqgjr,r->bqgj', imp[..., pool_idx], pool_w)
        cur = t // SEL_BLOCK
        causal_blk = blk[None, :] <= cur[:, None]
        forced = (blk[None, :] == 0) | (blk[None, :] == cur[:, None]) | (blk[None, :] == cur[:, None] - 1)
        score = jnp.where((forced & causal_blk)[None, :, None, :], FORCE,
                          jnp.where(causal_blk[None, :, None, :], imp_sel, NEG))
        top_val, top_idx = lax.top_k(score, top_n)
        blk_ok = top_val > 0.5 * NEG
        k_sel = k_sel_blocks[bi, gi, top_idx]
        v_sel = v_sel_blocks[bi, gi, top_idx]
        pos_s = top_idx[..., None] * SEL_BLOCK + jnp.arange(SEL_BLOCK)
        dist_s = t[None, :, None, None, None] - pos_s
        valid_s = blk_ok[..., None] & (dist_s >= 0)
        s_s = jnp.einsum('bqghd,bqgnld->bqghnl', qb, k_sel) - slopes[None, None, :, :, None, None] * dist_s[:, :, :, None]
        p_s = masked_softmax(s_s, valid_s[:, :, :, None], axis=(-2, -1))
        o_s = jnp.einsum('bqghnl,bqgnld->bqghd', p_s, v_sel)
        k_w = lax.dynamic_slice_in_dim(k_win_pad, start, WINDOW + QB, axis=1)
        v_w = lax.dynamic_slice_in_dim(v_win_pad, start, WINDOW + QB, axis=1)
        pos_w = start - WINDOW + jnp.arange(WINDOW + QB)
        dist_w = t[:, None] - pos_w[None, :]
        valid_w = (dist_w >= 0) & (dist_w < WINDOW) & (pos_w[None, :] >= 0)
        s_w = jnp.einsum('bqghd,bkgd->bqghk', qb, k_w) - slopes[None, None, :, :, None] * dist_w[None, :, None, None, :]
        p_w = masked_softmax(s_w, valid_w[None, :, None, None, :])
        o_w = jnp.einsum('bqghk,bkgd->bqghd', p_w, v_w)
        return gb[..., 0:1] * o_c + gb[..., 1:2] * o_s + gb[..., 2:3] * o_w

    outs = lax.map(block_fn, jnp.arange(S // QB))
    o = outs.transpose(1, 0, 2, 3, 4, 5).reshape(B, S, H * Dh).astype(h.dtype)
    return o @ w_out


def conv_ffn(h, w_gate_up, conv_w, conv_b, w_down):
    S = h.shape[1]
    gu = h @ w_gate_up
    g, u = gu[..., :D_FF], gu[..., D_FF:]
    gp = jnp.pad(g, ((0, 0), (CONV_WIDTH - 1, 0), (0, 0)))
    gc = conv_b
    for j in range(CONV_WIDTH):
        gc = gc + gp[:, j:j + S] * conv_w[j]
    return (jax.nn.silu(gc) * u) @ w_down


def setup_inputs(seed: int = 0) -> dict:
    key = jax.random.key(seed)
    ks = jax.random.split(key, 32)
    f32 = jnp.float32

    def w(k, shape, fan_in):
        return jax.random.normal(k, shape, f32) * (fan_in ** -0.5)

    def gain(k, shape):
        return 1.0 + 0.05 * jax.random.normal(k, shape, f32)

    return {
        "x": jax.random.normal(ks[0], (BATCH, SEQ, D_MODEL), f32),
        "p": jax.random.normal(ks[1], (DEPTH, BATCH, SEQ, PLE_DIM), f32),
        "mix_pre_norm": gain(ks[2], (DEPTH, D_MODEL)),
        "mix_post_norm": gain(ks[3], (DEPTH, D_MODEL)),
        "ffn_pre_norm": gain(ks[4], (DEPTH, D_MODEL)),
        "ffn_post_norm": gain(ks[5], (DEPTH, D_MODEL)),
        "ml_w_in": w(ks[6], (N_MLSTM_LAYERS, D_MODEL, ML_IN_COLS), D_MODEL),
        "ml_b_if": jnp.concatenate([0.1 * jax.random.normal(ks[7], (N_MLSTM_LAYERS, ML_HEADS), f32),
                                     3.0 + 3.0 * jax.random.uniform(ks[8], (N_MLSTM_LAYERS, ML_HEADS), f32)], axis=-1),
        "ml_head_norm": gain(ks[9], (N_MLSTM_LAYERS, D_MODEL)),
        "ml_w_out": w(ks[10], (N_MLSTM_LAYERS, D_MODEL, D_MODEL), D_MODEL),
        "nsa_w_in": w(ks[11], (N_NSA_LAYERS, D_MODEL, NSA_IN_COLS), D_MODEL),
        "nsa_cmp_pe": 0.1 * jax.random.normal(ks[12], (N_NSA_LAYERS, 2, CMP_BLOCK, NSA_HEAD_DIM), f32),
        "nsa_cmp_w1": w(ks[13], (N_NSA_LAYERS, 2, CMP_BLOCK, NSA_HEAD_DIM, CMP_HIDDEN), CMP_BLOCK * NSA_HEAD_DIM),
        "nsa_cmp_w2": w(ks[14], (N_NSA_LAYERS, 2, CMP_HIDDEN, NSA_HEAD_DIM), CMP_HIDDEN),
        "nsa_w_out": w(ks[15], (N_NSA_LAYERS, D_MODEL, D_MODEL), D_MODEL),
        "ffn_w_gate_up": w(ks[16], (DEPTH, D_MODEL, 2 * D_FF), D_MODEL),
        "ffn_conv_w": w(ks[17], (DEPTH, CONV_WIDTH, D_FF), CONV_WIDTH),
        "ffn_conv_b": 0.01 * jax.random.normal(ks[18], (DEPTH, D_FF), f32),
        "ffn_w_down": w(ks[19], (DEPTH, D_FF, D_MODEL), D_FF),
        "ple_w_proj": w(ks[20], (DEPTH, PLE_DIM, D_MODEL), PLE_DIM),
        "ple_norm": gain(ks[21], (DEPTH, D_MODEL)),
        "ple_w_gate": w(ks[22], (DEPTH, D_MODEL, D_MODEL), D_MODEL),
    }


def reference(x, p, mix_pre_norm, mix_post_norm, ffn_pre_norm, ffn_post_norm,
              ml_w_in, ml_b_if, ml_head_norm, ml_w_out,
              nsa_w_in, nsa_cmp_pe, nsa_cmp_w1, nsa_cmp_w2, nsa_w_out,
              ffn_w_gate_up, ffn_conv_w, ffn_conv_b, ffn_w_down,
              ple_w_proj, ple_norm, ple_w_gate):
    for i in range(DEPTH):
        j = i // N_MIXERS
        h = rms_norm(x, mix_pre_norm[i])
        if i % N_MIXERS == 0:
            h = mlstm_mixer(h, ml_w_in[j], ml_b_if[j], ml_head_norm[j], ml_w_out[j])
        else:
            h = nsa_mixer(h, nsa_w_in[j], nsa_cmp_pe[j], nsa_cmp_w1[j], nsa_cmp_w2[j], nsa_w_out[j])
        x = x + rms_norm(h, mix_post_norm[i])
        h = rms_norm(x, ffn_pre_norm[i])
        h = conv_ffn(h, ffn_w_gate_up[i], ffn_conv_w[i], ffn_conv_b[i], ffn_w_down[i])
        x = x + rms_norm(h, ffn_post_norm[i])
        ple = rms_norm(p[i] @ ple_w_proj[i], ple_norm[i])
        gate = jax.nn.sigmoid((x @ ple_w_gate[i]).astype(jnp.float32)).astype(x.dtype)
        x = x + gate * ple
    return x
```

```python
import math
from contextlib import ExitStack
import numpy as np
import ml_dtypes
import concourse.bass as bass
import concourse.mybir as mybir
from concourse.bass_utils import run_bass_kernel_spmd

F32 = mybir.dt.float32
BF16 = mybir.dt.bfloat16
AF = mybir.ActivationFunctionType
ALU = mybir.AluOpType
AX = mybir.AxisListType


class Buf:
    def __init__(self, name, dram=False):
        self.name = name
        self.dram = dram
        self.last_write = None
        self.readers = []
        self.sem = None
        self.count = 0


class Prog:
    ENGS = ("pe", "act", "dve", "pool", "sp")

    def __init__(self, nc):
        self.nc = nc
        self.ops = {e: [] for e in self.ENGS}
        self.eng_count = {e: 0 for e in self.ENGS}
        self.waited = {e: {} for e in self.ENGS}
        self.sem_names = []
        self.nbuf = 0

    def reg(self, e, val):
        if not hasattr(self, "_regs"):
            self._regs = {}
        if val not in self._regs:
            self._regs[val] = e.to_reg(val)
        return self._regs[val]

    def buf(self, name=None, dram=False):
        self.nbuf += 1
        return Buf(name or f"b{self.nbuf}", dram)

    def _newsem(self, name):
        self.sem_names.append(name)
        return name

    def _deps(self, eng, reads, writes, is_dma):
        deps = []
        for b in reads:
            if b.last_write is not None:
                deps.append(b.last_write)
        for b in writes:
            if b.last_write is not None:
                deps.append(b.last_write)
            deps.extend(b.readers)
        out = {}
        for (s, v, e_src, dma_src) in deps:
            if (not dma_src) and (not is_dma) and e_src == eng and eng == "pe":
                continue
            if out.get(s, 0) < v:
                out[s] = v
        w = self.waited[eng]
        res = []
        for s, v in out.items():
            if w.get(s, 0) < v:
                w[s] = v
                res.append((s, v))
        return res

    def op(self, eng, emit, reads=(), writes=(), inc=True):
        reads = [b for b in reads if b is not None]
        writes = [b for b in writes if b is not None]
        waits = self._deps_compute(eng, reads, writes)
        if inc:
            self.eng_count[eng] += 1
            tok = ("E_" + eng, self.eng_count[eng], eng, False)
        else:
            tok = ("E_" + eng, self.eng_count[eng] + 1, eng, False)
        for b in reads:
            b.readers.append(tok)
        for b in writes:
            b.last_write = tok
            b.readers = []
        self.ops[eng].append((emit, waits, ("E_" + eng, 1) if inc else None))

    def _deps_compute(self, eng, reads, writes):
        deps = []
        for b in reads:
            if b.last_write is not None:
                deps.append(b.last_write)
        for b in writes:
            if b.last_write is not None and not (b.last_write[2] == eng and not b.last_write[3]):
                deps.append(b.last_write)
            for r in b.readers:
                if not (r[2] == eng and not r[3]):
                    deps.append(r)
        out = {}
        for (s, v, e_src, dma_src) in deps:
            if eng == "pe" and e_src == "pe" and not dma_src:
                continue
            if out.get(s, 0) < v:
                out[s] = v
        w = self.waited[eng]
        res = []
        for s, v in out.items():
            if w.get(s, 0) < v:
                w[s] = v
                res.append((s, v))
        return res

    def dma(self, eng, emit, reads=(), writes=(), sembuf=None):
        reads = [b for b in reads if b is not None]
        writes = [b for b in writes if b is not None]
        deps = []
        for b in reads:
            if b.last_write is not None:
                deps.append(b.last_write)
        for b in writes:
            if b.last_write is not None:
                deps.append(b.last_write)
            deps.extend(b.readers)
        out = {}
        for (s, v, e_src, dma_src) in deps:
            if out.get(s, 0) < v:
                out[s] = v
        w = self.waited[eng]
        waits = []
        for s, v in out.items():
            if w.get(s, 0) < v:
                w[s] = v
                waits.append((s, v))
        cand = [b for b in (writes + reads) if not b.dram]
        sb = sembuf or (cand[0] if cand else (writes + reads)[0])
        if sb.sem is None:
            sb.sem = self._newsem("D_" + sb.name)
        sb.count += 16
        tok = (sb.sem, sb.count, eng, True)
        for b in reads:
            b.readers.append(tok)
        for b in writes:
            b.last_write = tok
            b.readers = []
        self.ops[eng].append((emit, waits, (sb.sem, 16)))

    def finish(self, bufs):
        waits = {}
        for b in bufs:
            toks = list(b.readers)
            if b.last_write is not None:
                toks.append(b.last_write)
            for (s, v, _, _) in toks:
                if waits.get(s, 0) < v:
                    waits[s] = v
        self.ops["sp"].append((None, list(waits.items()), None))

    def emit(self, stack):
        nc = self.nc
        sems = {}
        for e in self.ENGS:
            if self.eng_count[e] > 0:
                sems["E_" + e] = stack.enter_context(nc.semaphore("E_" + e))
        for n in self.sem_names:
            sems[n] = stack.enter_context(nc.semaphore(n))
        block = stack.enter_context(nc.Block())
        ops = self.ops

        def run(eng_obj, lst):
            for (emit, waits, inc) in lst:
                for (s, v) in waits:
                    eng_obj.wait_ge(sems[s], v)
                if emit is None:
                    continue
                ins = emit(eng_obj)
                if inc is not None:
                    ins.then_inc(sems[inc[0]], inc[1])

        if ops["sp"]:
            @block.sync
            def _(e):
                run(e, ops["sp"])
        if ops["pe"]:
            @block.tensor
            def _(e):
                run(e, ops["pe"])
        if ops["act"]:
            @block.scalar
            def _(e):
                run(e, ops["act"])
        if ops["dve"]:
            @block.vector
            def _(e):
                run(e, ops["dve"])
        if ops["pool"]:
            @block.gpsimd
            def _(e):
                run(e, ops["pool"])


def _mk(P):
    def DMA(q, out, in_, reads=(), writes=(), sembuf=None):
        P.dma(q, lambda e: e.dma_start(out=out, in_=in_), reads=reads, writes=writes, sembuf=sembuf)

    def MM(out, lhsT, rhs, start, stop, reads=(), writes=(), inc=True):
        P.op("pe", lambda e: e.matmul(out, lhsT, rhs, start=start, stop=stop), reads=reads, writes=writes, inc=inc)

    def ACTF(out, in_, func, reads=(), writes=(), **kw):
        P.op("act", lambda e: e.activation(out=out, in_=in_, func=func, **kw), reads=reads, writes=writes)

    def TT(eng, out, in0, in1, op, reads=(), writes=()):
        P.op(eng, lambda e: e.tensor_tensor(out, in0, in1, op), reads=reads, writes=writes)

    def TS(eng, out, in0, s1, s2, op0, op1=None, reads=(), writes=()):
        if op1 is None:
            P.op(eng, lambda e: e.tensor_scalar(out, in0, s1, s2, op0), reads=reads, writes=writes)
        else:
            P.op(eng, lambda e: e.tensor_scalar(out, in0, s1, s2, op0, op1), reads=reads, writes=writes)

    def STT(out, in0, scalar, in1, op0, op1, reads=(), writes=()):
        P.op("dve", lambda e: e.scalar_tensor_tensor(out, in0, scalar, in1, op0, op1), reads=reads, writes=writes)

    def CP(eng, out, in_, reads=(), writes=()):
        if eng == "act":
            P.op("act", lambda e: e.copy(out, in_), reads=reads, writes=writes)
        else:
            P.op(eng, lambda e: e.tensor_copy(out, in_), reads=reads, writes=writes)

    def MEMSET(eng, out, val, writes=()):
        P.op(eng, lambda e: e.memset(out, val), writes=writes)

    return DMA, MM, ACTF, TT, TS, STT, CP, MEMSET


Prog.helpers = lambda self: _mk(self)


EPS = 1e-6


def chunks_of(n, c=128):
    out = []
    o = 0
    while o < n:
        w = min(c, n - o)
        out.append((o, w))
        o += w
    return out


class Ring:
    def __init__(self, P, nc, stack, name, n, shape, dt):
        self.t = [stack.enter_context(nc.sbuf_tensor(f"{name}{i}", shape, dt)) for i in range(n)]
        self.b = [P.buf(f"{name}{i}") for i in range(n)]
        self.i = 0
        self.n = n

    def next(self):
        r = (self.t[self.i % self.n], self.b[self.i % self.n])
        self.i += 1
        return r


def build_dense(D, DFF, PLE, mode="full", last=False, debug=False, PASSW=256):
    NT = 1026
    KC = D // 128
    TILES = [(0, 2), (2, 514), (514, 1026)]
    MAIN = TILES[1:]
    nc = bass.Bass("TRN2", target_bir_lowering=False)
    P = Prog(nc)
    DMA, MM, ACTF, TT, TS, STT, CP, MEMSET = P.helpers()
    dt = nc.dram_tensor

    def din(name, shape, dtype):
        return dt(name, shape, dtype, kind="ExternalInput").ap()

    def dout(name, shape, dtype):
        return dt(name, shape, dtype, kind="ExternalOutput").ap()

    FC = chunks_of(DFF)
    NFC = len(FC)
    xT = din("xT", [KC, 128, NT], F32)
    gvec = din("gvec", [128, 6, KC], F32)
    if mode == "full":
        oT = din("oT", [KC, 128, NT], BF16)
        w_out = din("w_out", [D, D], F32)
        w_gu = din("w_gu", [D, 2 * DFF], F32)
        convp = din("convp", [128, 4, NFC], F32)
        w_down = din("w_down", [DFF, D], F32)
        pT = din("pT", [PLE, 1024], F32)
        w_pp = din("w_pp", [PLE, D], F32)
        w_pg = din("w_pg", [D, D], F32)
        x3T = dout("x3T", [KC, 128, 1024], F32)
        Y = (dout if debug else (lambda *a: dt(*a).ap()))("Ys", [KC, 128, NT], F32)
        X1 = (dout if debug else (lambda *a: dt(*a).ap()))("X1s", [KC, 128, NT], F32)
        A = (dout if debug else (lambda *a: dt(*a).ap()))("As", [NFC, 128, 1024], BF16)
        X2 = (dout if debug else (lambda *a: dt(*a).ap()))("X2s", [KC, 128, 1024], F32)
        PL = (dout if debug else (lambda *a: dt(*a).ap()))("PLs", [KC, 128, 1024], F32)
        Yb = [P.buf(f"Y{c}", dram=True) for c in range(KC)]
        X1b = [P.buf(f"X1{c}", dram=True) for c in range(KC)]
        Ab = [P.buf(f"A{c}", dram=True) for c in range(NFC)]
        X2b = [P.buf(f"X2{c}", dram=True) for c in range(KC)]
        PLb = [P.buf(f"PL{c}", dram=True) for c in range(KC)]
        x3b = [P.buf(f"x3{c}", dram=True) for c in range(KC)]
    if not last:
        hTn = dout("hTn", [KC, 128, 1024], BF16)
        hnb = [P.buf(f"hn{c}", dram=True) for c in range(KC)]

    with ExitStack() as st:
        sb = lambda name, shape, dtype: st.enter_context(nc.sbuf_tensor(name, shape, dtype))
        hT = sb("hT", [128, KC, NT], BF16)
        hTb = [P.buf(f"hT{k}") for k in range(KC)]
        ones = sb("ones", [128, 128], F32)
        onesb = P.buf("ones")
        gv = sb("gv", [128, 6, KC], F32)
        gvb = P.buf("gv")
        rstd = sb("rstd", [128, NT], F32)
        rstdb = P.buf("rstd")
        cring = Ring(P, nc, st, "cr", 5, [128, NT], F32)
        sring = Ring(P, nc, st, "sq", 2, [128, NT], F32)
        WT = 256
        wring = Ring(P, nc, st, "wr", 3, [128, KC, WT], BF16)
        psum = [st.enter_context(nc.psum_tensor(f"ps{i}", [128, 512], F32)) for i in range(8)]
        psb = [P.buf(f"ps{i}") for i in range(8)]
        rot = [0]

        def nextbank():
            b = 3 + rot[0] % 5
            rot[0] += 1
            return b

        P.op("pool", lambda e: e.memset(ones[:], 1.0), writes=[onesb])
        epsc = sb("epsc", [128, 1], F32)
        epsb = P.buf("epsc")
        P.op("pool", lambda e: e.memset(epsc[:], EPS), writes=[epsb])
        DMA("sp", gv[:], gvec, writes=[gvb])
        if mode == "full":
            cv = sb("cv", [128, 4, NFC], F32)
            cvb = P.buf("cv")
            DMA("sp", cv[:], convp, writes=[cvb])

        evq = [0]

        def ev_eng():
            evq[0] += 1
            return "act" if evq[0] % 2 else "dve"

        def copy_op(eng, out, in_, reads, writes):
            if eng == "act":
                P.op("act", lambda e: e.copy(out, in_), reads=reads, writes=writes)
            else:
                P.op(eng, lambda e: e.tensor_copy(out, in_), reads=reads, writes=writes)

        def ssq_begin():
            return {"first": True}

        def ssq_add(state, src, srcb, tiles, lastflag):
            sq, sqb = sring.next()
            lo, hi = tiles[0][0], tiles[-1][1]
            P.op("act", lambda e: e.activation(out=sq[:, lo:hi], in_=src[:, lo:hi], func=AF.Square),
                 reads=[srcb], writes=[sqb])
            for (a, b_) in tiles:
                bank = {2: 0, 514: 1, 0: 2}[a]
                f = state["first"]
                P.op("pe", lambda e, a=a, b_=b_, bank=bank, f=f: e.matmul(
                    psum[bank][:, 0:b_ - a], ones[:], sq[:, a:b_], start=f, stop=lastflag),
                    reads=[sqb, onesb], writes=[psb[bank]])
            state["first"] = False

        def rstd_compute(tiles, dim):
            for (a, b_) in tiles:
                bank = {2: 0, 514: 1, 0: 2}[a]
                P.op("act", lambda e, a=a, b_=b_, bank=bank: e.activation(
                    out=rstd[:, a:b_], in_=psum[bank][:, 0:b_ - a], func=AF.Ln, scale=1.0 / dim, bias=epsc[:, 0:1]),
                    reads=[psb[bank], epsb], writes=[rstdb])
            lo, hi = tiles[0][0], tiles[-1][1]
            P.op("act", lambda e: e.activation(out=rstd[:, lo:hi], in_=rstd[:, lo:hi], func=AF.Exp, scale=-0.5),
                 reads=[rstdb], writes=[rstdb])

        def load_w(wdram, r0, c0, cw, kc_n, queue="pool"):
            wt, wb = wring.next()
            src = wdram[r0:r0 + kc_n * 128, c0:c0 + cw].rearrange("(k p) c -> p k c", p=128)
            step = 8
            for k0 in range(0, kc_n, step):
                k1 = min(kc_n, k0 + step)
                P.dma(queue, lambda e, k0=k0, k1=k1: e.dma_start(out=wt[:, k0:k1, 0:cw], in_=src[:, k0:k1, :]),
                      writes=[wb])
            return wt, wb

        def mm_cols(wt, wb, cc0, ccw, tiles, kc_n, inbufs=None):
            res = []
            for (a, b_) in tiles:
                bank = nextbank()
                for k in range(kc_n):
                    P.op("pe", lambda e, k=k, a=a, b_=b_, bank=bank: e.matmul(
                        psum[bank][0:ccw, 0:b_ - a], wt[:, k, cc0:cc0 + ccw], hT[:, k, a:b_],
                        start=(k == 0), stop=(k == kc_n - 1)),
                        reads=[wb, hTb[k]], writes=[psb[bank]], inc=(k == kc_n - 1))
                res.append((bank, a, b_))
            return res

        def norm_sweep_to_hT(src_dram, src_bufs, gi, tiles, src_cols):
            lo, hi = tiles[0][0], tiles[-1][1]
            for c in range(KC):
                ct, cb = cring.next()
                DMA("sp", ct[:, lo:hi], src_dram[c][:, lo - src_cols:hi - src_cols], reads=[src_bufs[c]], writes=[cb])
                P.op("dve", lambda e, c=c, ct=ct: e.scalar_tensor_tensor(
                    hT[:, c, lo:hi], ct[:, lo:hi], gv[:, gi, c:c + 1], rstd[:, lo:hi], ALU.mult, ALU.mult),
                    reads=[cb, gvb, rstdb], writes=[hTb[c]])

        if mode == "pre":
            xb = [P.buf(f"x{c}") for c in range(KC)]
            stt = ssq_begin()
            for c in range(KC):
                ct, cb = cring.next()
                DMA("sp", ct[:, 2:NT], xT[c][:, 2:NT], writes=[cb])
                ssq_add(stt, ct, cb, MAIN, c == KC - 1)
            rstd_compute(MAIN, D)
            for c in range(KC):
                ct, cb = cring.next()
                DMA("sp", ct[:, 2:NT], xT[c][:, 2:NT], writes=[cb])
                P.op("dve", lambda e, c=c, ct=ct: e.scalar_tensor_tensor(
                    hT[:, c, 2:NT], ct[:, 2:NT], gv[:, 4, c:c + 1], rstd[:, 2:NT], ALU.mult, ALU.mult),
                    reads=[cb, gvb, rstdb], writes=[hTb[c]])
                DMA("sp", hTn[c], hT[:, c, 2:NT], reads=[hTb[c]], writes=[hnb[c]])
            P.finish(hnb)
            P.emit(st)
            return nc

        for k in range(KC):
            DMA("sp", hT[:, k, :], oT[k], writes=[hTb[k]])

        def s_linear(wdram, ncols, tiles, kc_n, store):
            for (c0, cw) in chunks_of(ncols, WT):
                wt, wb = load_w(wdram, 0, c0, cw, kc_n)
                for (cc0, ccw) in chunks_of(cw):
                    res = mm_cols(wt, wb, cc0, ccw, tiles, kc_n)
                    store((c0 + cc0) // 128, ccw, res)

        def store_to(dst, dstb, coloff):
            def store(ci, ccw, res):
                ct, cb = cring.next()
                lo, hi = res[0][1], res[-1][2]
                for (bank, a, b_) in res:
                    copy_op(ev_eng(), ct[0:ccw, a:b_], psum[bank][0:ccw, 0:b_ - a], [psb[bank]], [cb])
                DMA("sp", dst[ci][0:ccw, lo - coloff:hi - coloff], ct[0:ccw, lo:hi], reads=[cb], writes=[dstb[ci]])
            return store

        s_linear(w_out, D, TILES, KC, store_to(Y, Yb, 0))

        def ssq_sweep(src, srcb, tiles, coloff, dim):
            lo, hi = tiles[0][0], tiles[-1][1]
            stt = ssq_begin()
            for c in range(KC):
                ct, cb = cring.next()
                DMA("sp", ct[:, lo:hi], src[c][:, lo - coloff:hi - coloff], reads=[srcb[c]], writes=[cb])
                ssq_add(stt, ct, cb, tiles, c == KC - 1)
            rstd_compute(tiles, dim)

        ssq_sweep(Y, Yb, TILES, 0, D)

        def resid_sweep(ysrc, ysrcb, xsrc, xsrcb, gi, tiles, ycoloff, xcoloff, dst, dstb, dcoloff, extra=None):
            lo, hi = tiles[0][0], tiles[-1][1]
            for c in range(KC):
                yt, yb = cring.next()
                DMA("sp", yt[:, lo:hi], ysrc[c][:, lo - ycoloff:hi - ycoloff], reads=[ysrcb[c]], writes=[yb])
                xt, xb_ = cring.next()
                DMA("sp", xt[:, lo:hi], xsrc[c][:, lo - xcoloff:hi - xcoloff], reads=[xsrcb[c] if xsrcb else None], writes=[xb_])
                P.op("dve", lambda e, c=c, yt=yt: e.scalar_tensor_tensor(
                    yt[:, lo:hi], yt[:, lo:hi], gv[:, gi, c:c + 1], rstd[:, lo:hi], ALU.mult, ALU.mult),
                    reads=[yb, gvb, rstdb], writes=[yb])
                P.op("pool", lambda e, yt=yt, xt=xt: e.tensor_tensor(xt[:, lo:hi], xt[:, lo:hi], yt[:, lo:hi], ALU.add),
                     reads=[yb, xb_], writes=[xb_])
                DMA("sp", dst[c][:, lo - dcoloff:hi - dcoloff], xt[:, lo:hi], reads=[xb_], writes=[dstb[c]])
                if extra:
                    extra(c, xt, xb_)

        resid_sweep(Y, Yb, xT, None, 0, TILES, 0, 0, X1, X1b, 0)
        ssq_sweep(X1, X1b, TILES, 0, D)
        norm_sweep_to_hT(X1, X1b, 1, TILES, 0)

        gbuf = sb("gbuf", [128, NT], F32)
        gbufb = P.buf("gbuf")
        tbuf = sb("tbuf", [128, 1024], F32)
        tbufb = P.buf("tbuf")
        aring = Ring(P, nc, st, "ar", 3, [128, 1024], BF16)
        for (c0, cw) in chunks_of(DFF, WT):
            wg, wgb = load_w(w_gu, 0, c0, cw, KC)
            wu, wub = load_w(w_gu, 0, DFF + c0, cw, KC)
            for (cc0, ccw) in chunks_of(cw):
                fi = (c0 + cc0) // 128
                rg = mm_cols(wg, wgb, cc0, ccw, TILES, KC)
                ru = mm_cols(wu, wub, cc0, ccw, MAIN, KC)
                for (bank, a, b_) in rg:
                    copy_op("act", gbuf[0:ccw, a:b_], psum[bank][0:ccw, 0:b_ - a], [psb[bank]], [gbufb])
                P.op("dve", lambda e, fi=fi, ccw=ccw: e.tensor_scalar(
                    tbuf[0:ccw, :], gbuf[0:ccw, 0:1024], cv[0:ccw, 0, fi:fi + 1], cv[0:ccw, 3, fi:fi + 1], ALU.mult, ALU.add),
                    reads=[gbufb, cvb], writes=[tbufb])
                P.op("dve", lambda e, fi=fi, ccw=ccw: e.scalar_tensor_tensor(
                    tbuf[0:ccw, :], gbuf[0:ccw, 1:1025], cv[0:ccw, 1, fi:fi + 1], tbuf[0:ccw, :], ALU.mult, ALU.add),
                    reads=[gbufb, cvb, tbufb], writes=[tbufb])
                P.op("dve", lambda e, fi=fi, ccw=ccw: e.scalar_tensor_tensor(
                    tbuf[0:ccw, :], gbuf[0:ccw, 2:1026], cv[0:ccw, 2, fi:fi + 1], tbuf[0:ccw, :], ALU.mult, ALU.add),
                    reads=[gbufb, cvb, tbufb], writes=[tbufb])
                P.op("act", lambda e, ccw=ccw: e.activation(out=tbuf[0:ccw, :], in_=tbuf[0:ccw, :], func=AF.Silu),
                     reads=[tbufb], writes=[tbufb])
                at, ab = aring.next()
                for (bank, a, b_) in ru:
                    P.op("dve", lambda e, bank=bank, a=a, b_=b_, at=at, ccw=ccw: e.tensor_tensor(
                        at[0:ccw, a - 2:b_ - 2], tbuf[0:ccw, a - 2:b_ - 2], psum[bank][0:ccw, 0:b_ - a], ALU.mult),
                        reads=[tbufb, psb[bank]], writes=[ab])
                DMA("sp", A[fi][0:ccw, :], at[0:ccw, :], reads=[ab], writes=[Ab[fi]])

        wdring = Ring(P, nc, st, "wd", 6, [128, 256], BF16)
        adring = Ring(P, nc, st, "ad", 4, [128, 1024], BF16)
        for (p0, pw) in chunks_of(D, PASSW):
            ccs = chunks_of(pw)
            banks = {}
            bi = 2
            for ci in range(len(ccs)):
                for ti in range(2):
                    banks[(ci, ti)] = bi
                    bi += 1
            for fi, (f0, fw) in enumerate(FC):
                adt, adb = adring.next()
                DMA("sp", adt[0:fw, :], A[fi][0:fw, :], reads=[Ab[fi]], writes=[adb])
                wdt, wdb = wdring.next()
                DMA("pool", wdt[0:fw, 0:pw], w_down[f0:f0 + fw, p0:p0 + pw], writes=[wdb])
                for ci, (cc0, ccw) in enumerate(ccs):
                    for ti in range(2):
                        bank = banks[(ci, ti)]
                        lastmm = (ci == len(ccs) - 1 and ti == 1)
                        P.op("pe", lambda e, bank=bank, cc0=cc0, ccw=ccw, ti=ti, fw=fw, wdt=wdt, adt=adt, fi=fi: e.matmul(
                            psum[bank][0:ccw, :], wdt[0:fw, cc0:cc0 + ccw], adt[0:fw, ti * 512:(ti + 1) * 512],
                            start=(fi == 0), stop=(fi == NFC - 1)),
                            reads=[wdb, adb], writes=[psb[bank]], inc=lastmm)
            for ci, (cc0, ccw) in enumerate(ccs):
                ct, cb = cring.next()
                for ti in range(2):
                    bank = banks[(ci, ti)]
                    copy_op(ev_eng(), ct[0:ccw, 2 + ti * 512:2 + (ti + 1) * 512], psum[bank][0:ccw, :], [psb[bank]], [cb])
                cidx = (p0 + cc0) // 128
                DMA("sp", Y[cidx][:, 2:NT], ct[:, 2:NT], reads=[cb], writes=[Yb[cidx]])

        ssq_sweep(Y, Yb, MAIN, 0, D)
        def to_hT(c, xt, xb_):
            P.op("act", lambda e, c=c, xt=xt: e.copy(hT[:, c, 2:NT], xt[:, 2:NT]), reads=[xb_], writes=[hTb[c]])
        resid_sweep(Y, Yb, X1, X1b, 2, MAIN, 0, 0, X2, X2b, 2, extra=to_hT)

        KP = PLE // 128
        pTs = sb("pTs", [128, KP, 1024], BF16)
        pTb = P.buf("pTs")
        DMA("pool", pTs[:], pT.rearrange("(k p) t -> p k t", p=128), writes=[pTb])
        for (c0, cw) in chunks_of(D, WT):
            wt, wb = load_w(w_pp, 0, c0, cw, KP)
            for (cc0, ccw) in chunks_of(cw):
                ct, cb = cring.next()
                for ti in range(2):
                    bank = nextbank()
                    for k in range(KP):
                        P.op("pe", lambda e, k=k, bank=bank, ti=ti, wt=wt, cc0=cc0, ccw=ccw: e.matmul(
                            psum[bank][0:ccw, :], wt[:, k, cc0:cc0 + ccw], pTs[:, k, ti * 512:(ti + 1) * 512],
                            start=(k == 0), stop=(k == KP - 1)), reads=[wb, pTb], writes=[psb[bank]], inc=(k == KP - 1))
                    copy_op(ev_eng(), ct[0:ccw, 2 + ti * 512:2 + (ti + 1) * 512], psum[bank][0:ccw, :], [psb[bank]], [cb])
                cidx = (c0 + cc0) // 128
                DMA("sp", PL[cidx], ct[:, 2:NT], reads=[cb], writes=[PLb[cidx]])
        ssq_sweep(PL, PLb, MAIN, 2, D)

        def s8_store(ci, ccw, res):
            gt, gb = cring.next()
            for (bank, a, b_) in res:
                P.op("act", lambda e, bank=bank, a=a, b_=b_, gt=gt: e.activation(
                    out=gt[:, a:b_], in_=psum[bank][:, 0:b_ - a], func=AF.Sigmoid), reads=[psb[bank]], writes=[gb])
            pt, pb = cring.next()
            DMA("sp", pt[:, 2:NT], PL[ci], reads=[PLb[ci]], writes=[pb])
            xt, xb_ = cring.next()
            DMA("sp", xt[:, 2:NT], X2[ci], reads=[X2b[ci]], writes=[xb_])
            P.op("dve", lambda e, pt=pt: e.scalar_tensor_tensor(
                pt[:, 2:NT], pt[:, 2:NT], gv[:, 3, ci:ci + 1], rstd[:, 2:NT], ALU.mult, ALU.mult),
                reads=[pb, gvb, rstdb], writes=[pb])
            P.op("pool", lambda e, pt=pt, gt=gt: e.tensor_tensor(pt[:, 2:NT], pt[:, 2:NT], gt[:, 2:NT], ALU.mult),
                 reads=[pb, gb], writes=[pb])
            P.op("dve", lambda e, pt=pt, xt=xt: e.tensor_tensor(xt[:, 2:NT], xt[:, 2:NT], pt[:, 2:NT], ALU.add),
                 reads=[pb, xb_], writes=[xb_])
            DMA("sp", x3T[ci], xt[:, 2:NT], reads=[xb_], writes=[x3b[ci]])

        s_linear(w_pg, D, MAIN, KC, s8_store)

        if not last:
            ssq_sweep(x3T, x3b, MAIN, 2, D)
            norm_sweep_to_hT(x3T, x3b, 4, MAIN, 2)
            for c in range(KC):
                DMA("sp", hTn[c], hT[:, c, 2:NT], reads=[hTb[c]], writes=[hnb[c]])
            P.finish(hnb + x3b)
        else:
            P.finish(x3b)
        P.emit(st)
    return nc


def build_mlstm(D=4096, S=8192, DK=256, DV=512, debug=False):
    KC = D // 128
    NQ = DK // 128
    NV = DV // 128
    TT = 512
    NTT = S // TT
    NSC = S // 128
    nc = bass.Bass("TRN2", target_bir_lowering=False)
    P = Prog(nc)
    DMA, MM, ACTF, TTo, TS, STT, CP, MEMSET = P.helpers()
    dt = nc.dram_tensor
    din = lambda n, s, d: dt(n, s, d, kind="ExternalInput").ap()
    dout = lambda n, s, d: dt(n, s, d, kind="ExternalOutput").ap()
    scr = (dout if debug else (lambda n, s, d: dt(n, s, d).ap()))
    hT = din("hT", [KC, 128, S], BF16)
    wq = din("wq", [D, DK], F32)
    wk = din("wk", [D, DK], F32)
    wv = din("wv", [D, DV], F32)
    wo = din("wo", [D, DV], F32)
    wif = din("wif", [D, 2], F32)
    bif = din("bif", [2, 1], F32)
    hn_in = din("hn", [128, NV], F32)
    selB_in = din("selB", [2, 128], F32)
    c12_in = din("c12", [2, 2], F32)
    oT = dout("oT", [NV, 128, S], BF16)
    qTs = scr("qTs", [NQ, 128, S], BF16)
    kTs = scr("kTs", [NQ, 128, S], BF16)
    sgs = scr("sgs", [NV, 128, S], BF16)
    Vs = scr("Vs", [NSC, 128, DV], BF16)
    qb = P.buf("qTs", dram=True)
    kb = P.buf("kTs", dram=True)
    sgb = P.buf("sgs", dram=True)
    vb = P.buf("Vs", dram=True)
    ob = P.buf("oT", dram=True)

    with ExitStack() as st:
        sb = lambda name, shape, dtype: st.enter_context(nc.sbuf_tensor(name, shape, dtype))
        RW = sb("RW", [128, 49152], BF16)
        RWf = sb
        rwb = P.buf("RW")
        wq_s = RW[:, 0:KC * DK].rearrange("p (k c) -> p k c", k=KC)
        wk_s = RW[:, KC * DK:2 * KC * DK].rearrange("p (k c) -> p k c", k=KC)
        wv_s = RW[:, 2 * KC * DK:2 * KC * DK + KC * DV].rearrange("p (k c) -> p k c", k=KC)
        wo_s = RW[:, 2 * KC * DK + KC * DV:2 * KC * DK + 2 * KC * DV].rearrange("p (k c) -> p k c", k=KC)
        assert 2 * KC * DK + 2 * KC * DV <= 49152
        wif_s = sb("wif_s", [128, KC, 2], BF16)
        wifb = P.buf("wif")
        BtAll = sb("BtAll", [128, S], F32)
        btb = P.buf("BtAll")
        consts = sb("consts", [128, 8], F32)
        cb_ = P.buf("consts")
        hn = sb("hn_s", [128, NV], F32)
        selB = sb("selB_s", [2, 128], F32)
        c12 = sb("c12_s", [2, 2], F32)
        bifs = sb("bif_s", [2, 1], F32)
        onesf = sb("onesf", [128, 128], F32)
        onesb16 = sb("onesb16", [128, 128], BF16)
        colb = sb("colb", [128, NSC], F32)
        colbb = P.buf("colb")
        psum = [st.enter_context(nc.psum_tensor(f"ps{i}", [128, 512], F32)) for i in range(8)]
        psb = [P.buf(f"ps{i}") for i in range(8)]

        MEMSET("pool", onesf[:], 1.0, writes=[cb_])
        MEMSET("pool", onesb16[:], 1.0, writes=[cb_])
        MEMSET("pool", consts[:, 0:1], 1.0, writes=[cb_])
        MEMSET("pool", consts[:, 1:2], EPS, writes=[cb_])
        DMA("sp", hn[:], hn_in, writes=[cb_])
        DMA("sp", selB[:], selB_in, writes=[cb_])
        DMA("sp", c12[:], c12_in, writes=[cb_])
        DMA("sp", bifs[:], bif, writes=[cb_])

        def loadw(dst, src, ncols):
            v = src.rearrange("(k p) c -> p k c", p=128)
            for k0 in range(0, KC, 8):
                DMA("pool", dst[:, k0:min(KC, k0 + 8), :], v[:, k0:min(KC, k0 + 8), :], writes=[rwb])
        loadw(wq_s, wq, DK)
        loadw(wk_s, wk, DK)
        loadw(wv_s, wv, DV)
        loadw(wo_s, wo, DV)
        DMA("pool", wif_s[:], wif.rearrange("(k p) c -> p k c", p=128), writes=[wifb])

        TW = 256
        hring = Ring(P, nc, st, "hr", 2, [128, KC, TW], BF16)
        ering = Ring(P, nc, st, "er", 4, [128, 512], BF16)
        rot = [0]

        def nb():
            b = rot[0] % 8
            rot[0] += 1
            return b
        evq = [0]

        def ev():
            evq[0] += 1
            return "act" if evq[0] % 2 else "dve"
        hv = hT.rearrange("k p t -> p k t")
        graw = BtAll[0:2, :]
        for ti in range(S // TW):
            t0 = ti * TW
            ht, hb = hring.next()
            for k0 in range(0, KC, 8):
                DMA("sp", ht[:, k0:min(KC, k0 + 8), :], hv[:, k0:min(KC, k0 + 8), t0:t0 + TW], writes=[hb])
            for (ws, dst, dbuf) in ((wq_s, qTs, qb), (wk_s, kTs, kb)):
                for j in range(NQ):
                    bank = nb()
                    for k in range(KC):
                        MM(psum[bank][:, 0:TW], ws[:, k, j * 128:(j + 1) * 128], ht[:, k, :], k == 0, k == KC - 1,
                           reads=[rwb, hb], writes=[psb[bank]], inc=(k == KC - 1))
                    et, eb = ering.next()
                    CP(ev(), et[:, 0:TW], psum[bank][:, 0:TW], reads=[psb[bank]], writes=[eb])
                    DMA("sp", dst[j][:, t0:t0 + TW], et[:, 0:TW], reads=[eb], writes=[dbuf])
            for j in range(NV):
                bank = nb()
                for k in range(KC):
                    MM(psum[bank][:, 0:TW], wo_s[:, k, j * 128:(j + 1) * 128], ht[:, k, :], k == 0, k == KC - 1,
                       reads=[rwb, hb], writes=[psb[bank]], inc=(k == KC - 1))
                et, eb = ering.next()
                ACTF(et[:, 0:TW], psum[bank][:, 0:TW], AF.Sigmoid, reads=[psb[bank]], writes=[eb])
                DMA("sp", sgs[j][:, t0:t0 + TW], et[:, 0:TW], reads=[eb], writes=[sgb])
            for sub in range(TW // 128):
                bank = nb()
                for k in range(KC):
                    MM(psum[bank][:, 0:DV], ht[:, k, sub * 128:(sub + 1) * 128], wv_s[:, k, :], k == 0, k == KC - 1,
                       reads=[rwb, hb], writes=[psb[bank]], inc=(k == KC - 1))
                et, eb = ering.next()
                CP(ev(), et[:, 0:DV], psum[bank][:, 0:DV], reads=[psb[bank]], writes=[eb])
                DMA("sp", Vs[(t0 + sub * 128) // 128], et[:, 0:DV], reads=[eb], writes=[vb])
            bank = nb()
            for k in range(KC):
                MM(psum[bank][0:2, 0:TW], wif_s[:, k, :], ht[:, k, :], k == 0, k == KC - 1,
                   reads=[wifb, hb], writes=[psb[bank]], inc=(k == KC - 1))
            CP("dve", graw[:, t0:t0 + TW], psum[bank][0:2, 0:TW], reads=[psb[bank]], writes=[btb])

        class _V:
            def __init__(self, ap):
                self.ap = ap
            def __getitem__(self, k):
                return self.ap[k]
        Ta = _V(RW[0:2, 0:2 * S].bitcast(F32))
        Tb = _V(RW[0:2, 2 * S:4 * S].bitcast(F32))
        Tc = _V(RW[0:2, 4 * S:6 * S].bitcast(F32))
        assert 6 * S <= 49152
        tab, tbb, tcb = rwb, rwb, rwb
        bs = sb("bs", [2, 1], F32)
        bsb = P.buf("bs")
        TS("dve", bs[:], bifs[:], 1.0 / 15.0, None, ALU.mult, reads=[cb_], writes=[bsb])
        ACTF(Ta[:], graw, AF.Tanh, scale=1.0 / 15.0, bias=bs[:, 0:1], reads=[btb, bsb], writes=[tab])
        TS("dve", Ta[:], Ta[:], 15.0, None, ALU.mult, reads=[tab], writes=[tab])
        ACTF(Tb[:], Ta[:], AF.Exp, scale=-1.0, reads=[tab], writes=[tbb])
        ACTF(Tb[:], Tb[:], AF.Ln, bias=consts[0:2, 0:1], reads=[tbb, cb_], writes=[tbb])
        TS("dve", Tb[:], Tb[:], -1.0, None, ALU.mult, reads=[tbb], writes=[tbb])
        MEMSET("pool", Tc[:], 1.0, writes=[tcb])
        P.op("dve", lambda e: e.tensor_tensor_scan(graw, Tc[:], Tb[:], 0.0, ALU.mult, ALU.add),
             reads=[tcb, tbb], writes=[btb])
        cbank = nb()
        for sc in range(NSC):
            MM(psum[cbank][:, sc:sc + 1], Ta[:, sc * 128:(sc + 1) * 128], c12[:, 0:1], True, False,
               reads=[tab, cb_], writes=[psb[cbank]], inc=False)
            MM(psum[cbank][:, sc:sc + 1], graw[:, sc * 128:(sc + 1) * 128], c12[:, 1:2], False, True,
               reads=[btb, cb_], writes=[psb[cbank]], inc=(sc == NSC - 1))
        TS("dve", colb[:], psum[cbank][:, 0:NSC], -math.log(DK ** 0.5), None, ALU.add, reads=[psb[cbank]], writes=[colbb])
        for tt in range(NTT):
            bank = nb()
            MM(psum[bank][:, :], selB[:], graw[:, tt * TT:(tt + 1) * TT], True, True, reads=[btb, cb_], writes=[psb[bank]])
            CP(ev(), BtAll[:, tt * TT:(tt + 1) * TT], psum[bank][:, :], reads=[psb[bank]], writes=[btb])

        kT_s = RW[:, 0:NQ * S].rearrange("p (k t) -> p k t", k=NQ)
        V_s = RW[:, NQ * S:NQ * S + NSC * DV].rearrange("p (c d) -> p c d", c=NSC)
        assert NQ * S + NSC * DV <= 49152
        kv = kTs.rearrange("k p t -> p k t")
        for k in range(NQ):
            DMA("sp", kT_s[:, k, :], kTs[k], reads=[kb], writes=[rwb])
        vv = Vs.rearrange("c p d -> p c d")
        for c0 in range(0, NSC, 8):
            DMA("sp", V_s[:, c0:c0 + 8, :], vv[:, c0:c0 + 8, :], reads=[vb], writes=[rwb])

        qring = Ring(P, nc, st, "qr", 2, [128, NQ, TT], BF16)
        sgring = Ring(P, nc, st, "sg", 2, [128, NV, TT], BF16)
        dring = Ring(P, nc, st, "dr", 3, [128, TT], F32)
        pring = Ring(P, nc, st, "pr", 3, [128, TT], BF16)
        hc = sb("hc", [128, NV, TT], F32)
        hcb = P.buf("hc")
        rd = sb("rd", [128, TT], F32)
        rdb = P.buf("rd")
        sq = sb("sqh", [128, TT], F32)
        sqb = P.buf("sqh")
        osb = Ring(P, nc, st, "os", 2, [128, NV, TT], BF16)
        SB = [0, 1]
        HB = [2, 3, 4, 5]
        DB = 6
        XB = 7
        sq_i = [0]
        for tt in range(NTT):
            t0 = tt * TT
            qt, qtb = qring.next()
            for k in range(NQ):
                DMA("sp", qt[:, k, :], qTs[k][:, t0:t0 + TT], reads=[qb], writes=[qtb])
            sgt, sgtb = sgring.next()
            for j in range(NV):
                DMA("sp", sgt[:, j, :], sgs[j][:, t0:t0 + TT], reads=[sgb], writes=[sgtb])
            nsc = 4 * tt + 4
            for sc in range(nsc):
                s0 = sc * 128
                sbk = SB[sq_i[0] % 2]
                sq_i[0] += 1
                for k in range(NQ):
                    MM(psum[sbk][:, :], kT_s[:, k, s0:s0 + 128], qt[:, k, :], k == 0, k == NQ - 1,
                       reads=[rwb, qtb], writes=[psb[sbk]], inc=(k == NQ - 1))
                dtl, dtb = dring.next()
                ACTF(dtl[:], BtAll[:, t0:t0 + TT], AF.Exp, bias=colb[:, sc:sc + 1], reads=[btb, colbb], writes=[dtb])
                if s0 + 127 > t0:
                    P.op("pool", lambda e, dtl=dtl, base=t0 - s0: e.affine_select(
                        dtl[:], dtl[:], [[1, TT]], ALU.is_ge, P.reg(e, 0.0), base=base, channel_multiplier=-1),
                        reads=[dtb], writes=[dtb])
                pt, ptb = pring.next()
                TTo("dve", pt[:], psum[sbk][:, :], dtl[:], ALU.mult, reads=[psb[sbk], dtb], writes=[ptb])
                for j in range(NV):
                    MM(psum[HB[j]][:, :], V_s[:, sc, j * 128:(j + 1) * 128], pt[:], sc == 0, sc == nsc - 1,
                       reads=[rwb, ptb], writes=[psb[HB[j]]], inc=False)
                MM(psum[DB][:, :], onesb16[:], pt[:], sc == 0, sc == nsc - 1, reads=[cb_, ptb], writes=[psb[DB]])
            ACTF(rd[:], psum[DB][:, :], AF.Abs, reads=[psb[DB]], writes=[rdb])
            TS("dve", rd[:], rd[:], 1.0, None, ALU.max, reads=[rdb], writes=[rdb])
            P.op("dve", lambda e: e.reciprocal(rd[:], rd[:]), reads=[rdb], writes=[rdb])
            for j in range(NV):
                TTo("dve", hc[:, j, :], psum[HB[j]][:, :], rd[:], ALU.mult, reads=[psb[HB[j]], rdb], writes=[hcb])
            for j in range(NV):
                ACTF(sq[:], hc[:, j, :], AF.Square, reads=[hcb], writes=[sqb])
                MM(psum[XB][:, :], onesf[:], sq[:], j == 0, j == NV - 1, reads=[cb_, sqb], writes=[psb[XB]])
            ACTF(rd[:], psum[XB][:, :], AF.Ln, scale=1.0 / DV, bias=consts[:, 1:2], reads=[psb[XB], cb_], writes=[rdb])
            ACTF(rd[:], rd[:], AF.Exp, scale=-0.5, reads=[rdb], writes=[rdb])
            ot, otb = osb.next()
            for j in range(NV):
                STT(hc[:, j, :], hc[:, j, :], hn[:, j:j + 1], rd[:], ALU.mult, ALU.mult, reads=[hcb, cb_, rdb], writes=[hcb])
                TTo("pool", ot[:, j, :], hc[:, j, :], sgt[:, j, :], ALU.mult, reads=[hcb, sgtb], writes=[otb])
                DMA("sp", oT[j][:, t0:t0 + TT], ot[:, j, :], reads=[otb], writes=[ob])
        P.finish([ob])
        P.emit(st)
    return nc


NEG = -1e9
FORCE = 1e9
BIG = 30000.0


def split3(x):
    x = x.astype(np.float32)
    a = x.astype(ml_dtypes.bfloat16).astype(np.float32)
    r = x - a
    b = r.astype(ml_dtypes.bfloat16).astype(np.float32)
    c = (r - b).astype(ml_dtypes.bfloat16).astype(np.float32)
    return a, b, c


def nsa_consts(S, H_total, head0, perm=None):
    NTT = S // 512
    slopes = np.exp2(-8.0 * np.arange(1, H_total + 1, dtype=np.float32) / H_total).astype(np.float32)
    sl = slopes[head0:head0 + 8].astype(np.float64)
    if perm is not None:
        sl = sl[list(perm)]
    p = np.arange(128, dtype=np.float64)
    M0 = 4 * (NTT - 1)
    nm = M0 + 4
    AB = np.zeros((128, 8, nm), np.float32)
    for mi in range(nm):
        m = mi - M0
        AB[:, :, mi] = (sl[None, :] * (128.0 * m + p[:, None])).astype(np.float32)
    CB = np.zeros((128, 8, NTT), np.float32)
    for ci in range(NTT):
        m = ci - (NTT - 1)
        CB[:, :, ci] = (sl[None, :] * (512.0 * m + 16.0 * p[:, None] + 31.0)).astype(np.float32)
    dq = np.arange(512, dtype=np.float64)
    R = (-(sl[:, None] * dq[None, :]) * math.sqrt(128.0)).astype(np.float32)
    a, b, c = split3(R)
    rowb = np.stack([a, b, c], 0).astype(ml_dtypes.bfloat16)
    NSC = S // 128
    E = np.zeros((128, NSC, 128), np.float32)
    for sc in range(NSC):
        for half in range(2):
            if 2 * sc + half < 128:
                E[2 * sc + half, sc, half * 64:(half + 1) * 64] = 1
    E = E.astype(ml_dtypes.bfloat16)
    w = [1.0, 2.0, 2.0, 2.0, 1.0]
    PoolM = np.zeros((128, 4, 128), np.float32)
    for nc_ in range(4):
        for pp in range(128):
            n = nc_ * 128 + pp
            for j in range(128):
                r = n - 4 * j
                if 0 <= r <= 4:
                    PoolM[pp, nc_, j] = w[r]
    ident = np.eye(128, dtype=np.float32)
    SelG = np.zeros((24, 12, 128), np.float32)
    for r in range(12):
        SelG[r, r, :] = 1
    return dict(AB=AB, CB=CB, rowb=rowb, E=E, PoolM=PoolM, ident=ident, SelG=SelG)


def build_nsa(D=4096, S=8192, half_heads=4, debug=False):
    KC = D // 128
    NTT = S // 512
    NSC = S // 128
    NCMP = (S - 32) // 16 + 1
    NCC = (NCMP + 127) // 128
    M0 = 4 * (NTT - 1)
    SCL = 128.0 ** -0.5
    nc = bass.Bass("TRN2", target_bir_lowering=False)
    P = Prog(nc)
    DMA, MM, ACTF, TTo, TS, STT, CP, MEMSET = P.helpers()
    dt = nc.dram_tensor
    din = lambda n, s, d: dt(n, s, d, kind="ExternalInput").ap()
    dout = lambda n, s, d: dt(n, s, d, kind="ExternalOutput").ap()
    scr = (dout if debug else (lambda n, s, d: dt(n, s, d).ap()))
    hT = din("hT", [KC, 128, S], BF16)
    wq = din("wq", [D, 1024], F32)
    wkv = din("wkv", [D, 768], F32)
    wg = din("wg", [D, 24], F32)
    pe_in = din("pe", [2, 32, 128], F32)
    w1_in = din("w1", [2, 32, 128, 256], F32)
    w2_in = din("w2", [2, 256, 128], F32)
    AB_in = din("AB", [128, 8, M0 + 4], F32)
    CB_in = din("CB", [128, 8, NTT], F32)
    rowb_in = din("rowb", [3, 8, 512], BF16)
    E_in = din("E", [128, NSC, 128], BF16)
    PoolM_in = din("PoolM", [128, 4, 128], F32)
    ident_in = din("ident", [128, 128], F32)
    SelG_in = din("SelG", [24, 12, 128], F32)
    oT = dout("oT", [half_heads, 128, S], BF16)
    QT = scr("QT", [8, 128, S], BF16)
    KVT = scr("KVT", [4, 128, S], BF16)
    VS = scr("VS", [2, NSC, 128, 128], BF16)
    GT = scr("GT", [24, S], F32)
    qTb, kvb, vsb, gtb, ob = [P.buf(n, dram=True) for n in ("QT", "KVT", "VS", "GT", "oT")]
    dbg = {}
    if debug:
        dbg["KC_"] = dout("dKcT", [128, 512], BF16)
        dbg["VC_"] = dout("dVcS", [128, NCC, 128], BF16)
        dbg["sel"] = dout("dsel", [NTT, 4, 128, 128], F32)
        dbg["imp"] = dout("dimp", [NTT, 128, NCC, 512], F32)
        dbgb = P.buf("dbg", dram=True)

    with ExitStack() as st:
        sb = lambda name, shape, dtype: st.enter_context(nc.sbuf_tensor(name, shape, dtype))
        RWN = KC * (1024 + 768)
        RWN = max(RWN, 2 * S + 2 * 32 * 256, 2 * S + 3 * NSC * 128 + 8192 + 2 * 4 * 512 + 4096)
        RW = sb("RW", [128, RWN], BF16)
        rwb = P.buf("RW")
        wq_s = RW[:, 0:KC * 1024].rearrange("p (k c) -> p k c", k=KC)
        wkv_s = RW[:, KC * 1024:KC * 1792].rearrange("p (k c) -> p k c", k=KC)
        wg_s = sb("wg_s", [128, KC, 24], BF16)
        wgb = P.buf("wg")
        cst = P.buf("consts")
        AB = sb("AB_s", [128, 8, M0 + 4], F32)
        CB = sb("CB_s", [128, 8, NTT], F32)
        PoolM = sb("PoolM_s", [128, 4, 128], F32)
        ident = sb("ident_s", [128, 128], F32)
        SelG = sb("SelG_s", [24, 12, 128], F32)
        ones3 = sb("ones3", [3, 128], BF16)
        onesb16 = sb("onesb16", [128, 128], BF16)
        for (t_, s_) in ((AB, AB_in), (CB, CB_in), (PoolM, PoolM_in), (ident, ident_in), (SelG, SelG_in)):
            DMA("sp", t_[:], s_, writes=[cst])
        MEMSET("pool", ones3[:], 1.0, writes=[cst])
        MEMSET("pool", onesb16[:], 1.0, writes=[cst])
        psum = [st.enter_context(nc.psum_tensor(f"ps{i}", [128, 512], F32)) for i in range(8)]
        psb = [P.buf(f"ps{i}") for i in range(8)]
        rot = [0]

        def nb():
            b = rot[0] % 8
            rot[0] += 1
            return b
        evq = [0]

        def ev():
            evq[0] += 1
            return "act" if evq[0] % 2 else "dve"

        def loadw(dst, src):
            v = src.rearrange("(k p) c -> p k c", p=128)
            for k0 in range(0, KC, 8):
                k1 = min(KC, k0 + 8)
                DMA("pool", dst[:, k0:k1, :], v[:, k0:k1, :], writes=[rwb])
        loadw(wq_s, wq)
        loadw(wkv_s, wkv)
        DMA("pool", wg_s[:], wg.rearrange("(k p) c -> p k c", p=128), writes=[wgb])
        TW = 256
        HN = max(KC * TW, 2 * (NCC + half_heads) * 512)
        hraw = [sb(f"hr{i}", [128, HN], BF16) for i in range(2)]

        class _HR:
            t = [h_[:, 0:KC * TW].rearrange("p (k t) -> p k t", k=KC) for h_ in hraw]
            b = [P.buf("hr0"), P.buf("hr1")]
            i = 0

            def next(self):
                k = self.i % 2
                self.i += 1
                return self.t[k], self.b[k]
        hring = _HR()
        ering = Ring(P, nc, st, "er", 4, [128, 256], BF16)
        gring = Ring(P, nc, st, "gr", 2, [24, TW], F32)
        hv = hT.rearrange("k p t -> p k t")
        for ti in range(S // TW):
            t0 = ti * TW
            ht, hb = hring.next()
            for k0 in range(0, KC, 8):
                k1 = min(KC, k0 + 8)
                DMA("sp", ht[:, k0:k1, :], hv[:, k0:k1, t0:t0 + TW], writes=[hb])

            def fm(ws, c0, dst_ap, dbuf):
                bank = nb()
                for k in range(KC):
                    MM(psum[bank][:, 0:TW], ws[:, k, c0:c0 + 128], ht[:, k, :], k == 0, k == KC - 1,
                       reads=[rwb, hb], writes=[psb[bank]], inc=(k == KC - 1))
                et, eb = ering.next()
                CP(ev(), et[:, 0:TW], psum[bank][:, 0:TW], reads=[psb[bank]], writes=[eb])
                DMA("sp", dst_ap, et[:, 0:TW], reads=[eb], writes=[dbuf])
            for hh in range(8):
                fm(wq_s, hh * 128, QT[hh][:, t0:t0 + TW], qTb)
            for (i, j) in ((0, 0), (1, 1), (2, 2), (3, 4)):
                fm(wkv_s, j * 128, KVT[i][:, t0:t0 + TW], kvb)
            for (i, j) in ((0, 3), (1, 5)):
                for sub in range(TW // 128):
                    bank = nb()
                    for k in range(KC):
                        MM(psum[bank][:, 0:128], ht[:, k, sub * 128:(sub + 1) * 128], wkv_s[:, k, j * 128:(j + 1) * 128],
                           k == 0, k == KC - 1, reads=[rwb, hb], writes=[psb[bank]], inc=(k == KC - 1))
                    et, eb = ering.next()
                    CP(ev(), et[:, 0:128], psum[bank][:, 0:128], reads=[psb[bank]], writes=[eb])
                    DMA("sp", VS[i][(t0 + sub * 128) // 128], et[:, 0:128], reads=[eb], writes=[vsb])
            bank = nb()
            for k in range(KC):
                MM(psum[bank][0:24, 0:TW], wg_s[:, k, :], ht[:, k, :], k == 0, k == KC - 1,
                   reads=[wgb, hb], writes=[psb[bank]], inc=(k == KC - 1))
            gt_, gb_ = gring.next()
            ACTF(gt_[:], psum[bank][0:24, 0:TW], AF.Sigmoid, reads=[psb[bank]], writes=[gb_])
            DMA("sp", GT[:, t0:t0 + TW], gt_[:], reads=[gb_], writes=[gtb])

        rawT = RW[:, 0:2 * S].rearrange("p (j t) -> p j t", j=2)
        w1_s = RW[:, 2 * S:2 * S + 2 * 32 * 256].rearrange("p (j l e) -> p j l e", j=2, l=32)
        for j in range(2):
            DMA("sp", rawT[:, j, :], KVT[j], reads=[kvb], writes=[rwb])
            for l0 in range(0, 32, 8):
                DMA("pool", w1_s[:, j, l0:l0 + 8, :], w1_in[j][l0:l0 + 8].rearrange("l d e -> d l e"), writes=[rwb])
        w2_s = sb("w2_s", [128, 2, 2, 128], BF16)
        peT = sb("peT", [128, 2, 32], BF16)
        for j in range(2):
            DMA("pool", w2_s[:, j, :, :], w2_in[j].rearrange("(c p) d -> p c d", p=128), writes=[cst])
            P.dma("pool", lambda e, j=j: e.dma_start(out=peT[:, j, :], in_=pe_in[j].rearrange("l d -> d l"),
                                                      allow_slow_non_contiguous=True), writes=[cst])
        KcT = sb("KcT", [128, 512], BF16)
        VcS = sb("VcS", [128, NCC, 128], BF16)
        kcb = P.buf("KcT")
        MEMSET("pool", KcT[:], 0.0, writes=[kcb])
        MEMSET("pool", VcS[:], 0.0, writes=[kcb])
        hid = sb("hid", [128, 2, 512], BF16)
        hidb = P.buf("hid")
        MEMSET("pool", hid[:], 0.0, writes=[hidb])
        xx = sb("xx", [128, 512], F32)
        uu = sb("uu", [128, 512], F32)
        xb_ = P.buf("xx")
        cbias = sb("cbias", [128, 1], F32)
        for j in range(2):
            for ec in range(2):
                bank = nb()
                for l in range(32):
                    rhs = rawT[:, j, l:l + 16 * (NCMP - 1) + 1:16]
                    MM(psum[bank][:, 0:NCMP], w1_s[:, j, l, ec * 128:(ec + 1) * 128], rhs, l == 0, l == 31,
                       reads=[rwb], writes=[psb[bank]], inc=(l == 31))
                bank2 = nb()
                for l in range(32):
                    MM(psum[bank2][:, 0:1], w1_s[:, j, l, ec * 128:(ec + 1) * 128], peT[:, j, l:l + 1], l == 0, l == 31,
                       reads=[rwb, cst], writes=[psb[bank2]], inc=(l == 31))
                CP("dve", cbias[:], psum[bank2][:, 0:1], reads=[psb[bank2]], writes=[xb_])
                X = xx[:, 0:NCMP]
                U = uu[:, 0:NCMP]
                TS("dve", X, psum[bank][:, 0:NCMP], cbias[:, 0:1], None, ALU.add, reads=[psb[bank], xb_], writes=[xb_])
                TTo("dve", U, X, X, ALU.mult, reads=[xb_], writes=[xb_])
                TS("dve", U, U, 0.044715, 1.0, ALU.mult, ALU.add, reads=[xb_], writes=[xb_])
                TTo("dve", U, U, X, ALU.mult, reads=[xb_], writes=[xb_])
                ACTF(U, U, AF.Tanh, scale=0.7978845608028654, reads=[xb_], writes=[xb_])
                TS("dve", U, U, 0.5, 0.5, ALU.mult, ALU.add, reads=[xb_], writes=[xb_])
                TTo("dve", hid[:, ec, 0:NCMP], U, X, ALU.mult, reads=[xb_], writes=[hidb])
            if j == 0:
                bank = nb()
                for ec in range(2):
                    MM(psum[bank][:, 0:NCMP], w2_s[:, 0, ec, :], hid[:, ec, 0:NCMP], ec == 0, ec == 1,
                       reads=[cst, hidb], writes=[psb[bank]])
                CP("act", KcT[:, 0:NCMP], psum[bank][:, 0:NCMP], reads=[psb[bank]], writes=[kcb])
            else:
                for cc, (n0, nw) in enumerate(chunks_of(NCMP)):
                    bank = nb()
                    for ec in range(2):
                        MM(psum[bank][0:nw, 0:128], hid[:, ec, n0:n0 + nw], w2_s[:, 1, ec, :], ec == 0, ec == 1,
                           reads=[cst, hidb], writes=[psb[bank]])
                    CP("act", VcS[0:nw, cc, :], psum[bank][0:nw, 0:128], reads=[psb[bank]], writes=[kcb])
        if debug:
            DMA("sp", dbg["KC_"], KcT[:], reads=[kcb], writes=[dbgb])
            DMA("sp", dbg["VC_"], VcS[:], reads=[kcb], writes=[dbgb])

        KsT = RW[:, 0:S]
        KwT = RW[:, S:2 * S]
        Vsl = RW[:, 2 * S:2 * S + NSC * 128].rearrange("p (c d) -> p c d", c=NSC)
        Vwn = RW[:, 2 * S + NSC * 128:2 * S + 2 * NSC * 128].rearrange("p (c d) -> p c d", c=NSC)
        Et = RW[:, 2 * S + 2 * NSC * 128:2 * S + 3 * NSC * 128].rearrange("p (c k) -> p c k", c=NSC)
        DMA("sp", KsT, KVT[2], reads=[kvb], writes=[rwb])
        DMA("sp", KwT, KVT[3], reads=[kvb], writes=[rwb])
        for c0 in range(0, NSC, 16):
            c1 = min(NSC, c0 + 16)
            DMA("sp", Vsl[:, c0:c1, :], VS[0][c0:c1].rearrange("c p d -> p c d"), reads=[vsb], writes=[rwb])
            DMA("sp", Vwn[:, c0:c1, :], VS[1][c0:c1].rearrange("c p d -> p c d"), reads=[vsb], writes=[rwb])
        DMA("sp", Et[:], E_in, writes=[rwb])
        rowb = RW[0:3, RWN - 4096:RWN].rearrange("p (h q) -> p h q", h=8)
        DMA("sp", rowb, rowb_in, writes=[rwb])

        class VRing:
            def __init__(self, views, names, guard):
                self.t = views
                self.b = [P.buf(n) for n in names]
                self.i = 0
                self.n = len(views)
                self.guard = guard

            def next(self):
                k = self.i % self.n
                first = self.i < self.n
                self.i += 1
                return self.t[k], self.b[k], first
        base2 = 2 * S + 3 * NSC * 128
        qviews = [RW[:, base2 + i * 4096:base2 + (i + 1) * 4096].rearrange("p (h q) -> p h q", h=8) for i in range(2)]
        qring = VRing(qviews, ["qr0", "qr1"], rwb)
        base3 = base2 + 8192
        pcviews = [RW[:, base3 + i * NCC * 512:base3 + (i + 1) * NCC * 512].rearrange("p (c q) -> p c q", c=NCC) for i in range(2)]
        pcring = VRing(pcviews, ["pc0", "pc1"], rwb)
        assert base3 + 2 * NCC * 512 <= RWN
        gtring = Ring(P, nc, st, "gt", 2, [24, 512], F32)
        pring = Ring(P, nc, st, "pr", 4, [128, 512], BF16)
        hflat = hraw[0][:, :].bitcast(F32)

        class _V:
            def __init__(self, ap):
                self.ap = ap

            def __getitem__(self, k):
                return self.ap[k]
        impT = _V(hflat[:, 0:NCC * 512].rearrange("p (c q) -> p c q", c=NCC))
        ocs = _V(hflat[:, NCC * 512:NCC * 512 + half_heads * 512].rearrange("p (h q) -> p h q", h=half_heads))
        impb = hring.b[0]
        ocb = hring.b[0]
        oacc = sb("oacc", [128, 512], F32)
        oaccb = P.buf("oacc")
        rdn = Ring(P, nc, st, "rdn", 2, [128, 512], F32)
        tmpf = Ring(P, nc, st, "tmpf", 2, [128, 512], F32)
        ist = sb("ist", [128, 512], F32)
        istb = P.buf("ist")
        score = sb("score", [128, 4, 128], F32)
        scb = P.buf("score")
        sc2 = sb("sc2", [128, 4, 128], F32)
        sc2b = P.buf("sc2")
        m8 = sb("m8", [128, 16], F32)
        m8b = P.buf("m8")
        nselT = sb("nselT", [128, 512], BF16)
        nsb = P.buf("nselT")
        oout = Ring(P, nc, st, "oo", 2, [128, 512], BF16)

        def mask_cmp(pt, ptb, t0, ncx):
            base = t0 - 2048 * ncx - 31
            P.op("pool", lambda e: e.affine_select(pt[:], pt[:], [[1, 512]], ALU.is_ge, P.reg(e, 0.0), base=base, channel_multiplier=-16),
                 reads=[ptb], writes=[ptb])

        def mask_causal(pt, ptb, t0, sc):
            base = t0 - 128 * sc
            P.op("pool", lambda e: e.affine_select(pt[:], pt[:], [[1, 512]], ALU.is_ge, P.reg(e, 0.0), base=base, channel_multiplier=-1),
                 reads=[ptb], writes=[ptb])

        def mask_window(pt, ptb, t0, sc):
            base = 128 * sc - t0 + 511
            P.op("pool", lambda e: e.affine_select(pt[:], pt[:], [[-1, 512]], ALU.is_ge, P.reg(e, 0.0), base=base, channel_multiplier=1),
                 reads=[ptb], writes=[ptb])

        def gate_factor(gtile, gtb_, hh, x, den_bank):
            r = hh * 3 + x
            rt, rb = rdn.next()
            P.op("dve", lambda e: e.reciprocal(rt[:], psum[den_bank][:, :]), reads=[psb[den_bank]], writes=[rb])
            gbank = 7
            MM(psum[gbank][:, :], SelG[:, r, :], gtile[:], True, True, reads=[cst, gtb_], writes=[psb[gbank]])
            TTo("dve", rt[:], rt[:], psum[gbank][:, :], ALU.mult, reads=[rb, psb[gbank]], writes=[rb])
            return rt, rb

        for T in range(NTT):
            t0 = T * 512
            qt, qtb, qfirst = qring.next()
            for hh in range(8):
                DMA("sp", qt[:, hh, :], QT[hh][:, t0:t0 + 512], reads=[qTb] + ([rwb] if qfirst else []), writes=[qtb])
            gtile, gtlb = gtring.next()
            DMA("sp", gtile[:], GT[:, t0:t0 + 512], reads=[gtb], writes=[gtlb])
            ncs = [c for c in range(NCC) if 2048 * c <= t0 + 480]
            MEMSET("pool", impT[:], 0.0, writes=[impb])
            for hh in range(8):
                pc, pcb, pfirst = pcring.next()
                if pfirst:
                    MEMSET("pool", pc[:, 0, 0:1], 0.0, writes=[pcb, rwb])
                OB, DB = 2, 3
                for ci, c in enumerate(ncs):
                    sbk = ci % 2
                    MM(psum[sbk][:, :], KcT[:, c * 128:(c + 1) * 128], qt[:, hh, :], True, False,
                       reads=[kcb, qtb], writes=[psb[sbk]], inc=False)
                    MM(psum[sbk][:, :], ones3[:], rowb[:, hh, :], False, True, reads=[cst, rwb], writes=[psb[sbk]])
                    ACTF(pc[:, c, :], psum[sbk][:, :], AF.Exp, scale=SCL, bias=CB[:, hh, 4 * c - T + NTT - 1:4 * c - T + NTT],
                         reads=[psb[sbk], cst], writes=[pcb])
                    if not (2048 * c + 2063 <= t0):
                        P.op("pool", lambda e, pc=pc, c=c, base=t0 - 2048 * c - 31: e.affine_select(
                            pc[:, c, :], pc[:, c, :], [[1, 512]], ALU.is_ge, P.reg(e, 0.0), base=base, channel_multiplier=-16),
                            reads=[pcb], writes=[pcb])
                    if hh < half_heads:
                        MM(psum[OB][:, :], VcS[:, c, :], pc[:, c, :], ci == 0, ci == len(ncs) - 1,
                           reads=[kcb, pcb], writes=[psb[OB]], inc=False)
                    MM(psum[DB][:, :], onesb16[:], pc[:, c, :], ci == 0, ci == len(ncs) - 1,
                       reads=[cst, pcb], writes=[psb[DB]])
                rt, rb = rdn.next()
                TS("dve", rt[:], psum[DB][:, :], 1e-30, None, ALU.max, reads=[psb[DB]], writes=[rb])
                P.op("dve", lambda e, rt=rt: e.reciprocal(rt[:], rt[:]), reads=[rb], writes=[rb])
                for c in ncs:
                    tf, tfb = tmpf.next()
                    TTo("dve", tf[:], pc[:, c, :], rt[:], ALU.mult, reads=[pcb, rb], writes=[tfb])
                    TTo("pool", impT[:, c, :], impT[:, c, :], tf[:], ALU.add, reads=[tfb, impb], writes=[impb])
                if hh < half_heads:
                    gbank = 7
                    MM(psum[gbank][:, :], SelG[:, hh * 3 + 0, :], gtile[:], True, True, reads=[cst, gtlb], writes=[psb[gbank]])
                    TTo("dve", rt[:], rt[:], psum[gbank][:, :], ALU.mult, reads=[rb, psb[gbank]], writes=[rb])
                    TTo("dve", ocs[:, hh, :], psum[OB][:, :], rt[:], ALU.mult, reads=[psb[OB], rb], writes=[ocb])
            if debug:
                DMA("sp", dbg["imp"][T], impT[:], reads=[impb], writes=[dbgb])
            pbank = 4
            for c in range(NCC):
                MM(psum[pbank][:, :], PoolM[:, c, :], impT[:, c, :], c == 0, c == NCC - 1, reads=[cst, impb], writes=[psb[pbank]])
            CP("act", ist[:], psum[pbank][:, :], reads=[psb[pbank]], writes=[istb])
            for qc in range(4):
                tb = 5
                P.op("pe", lambda e, qc=qc: e.transpose(psum[tb][:, 0:128], ist[:, qc * 128:(qc + 1) * 128], ident[:]),
                     reads=[istb, cst], writes=[psb[tb]])
                CP("dve", score[:, qc, :], psum[tb][:, 0:128], reads=[psb[tb]], writes=[scb])
                tq = t0 + qc * 128
                P.op("pool", lambda e, qc=qc, tq=tq: e.affine_select(
                    score[:, qc, :], score[:, qc, :], [[-64, 128]], ALU.is_ge, P.reg(e, NEG), base=tq, channel_multiplier=1),
                    reads=[scb], writes=[scb])
                MEMSET("pool", score[:, qc, 0:1], FORCE, writes=[scb])
                for hf in range(2):
                    cur = tq // 64 + hf
                    lo = max(cur - 1, 0)
                    MEMSET("pool", score[hf * 64:(hf + 1) * 64, qc, lo:cur + 1], FORCE, writes=[scb])
                P.op("dve", lambda e, qc=qc: e.max(m8[:, 0:8], score[:, qc, :]), reads=[scb], writes=[m8b])
                P.op("dve", lambda e, qc=qc: e.match_replace(sc2[:, qc, :], m8[:, 0:8], score[:, qc, :], -3e9),
                     reads=[scb, m8b], writes=[sc2b])
                P.op("dve", lambda e, qc=qc: e.max(m8[:, 8:16], sc2[:, qc, :]), reads=[sc2b], writes=[m8b])
                TS("dve", sc2[:, qc, :], score[:, qc, :], m8[:, 15:16], None, ALU.is_ge, reads=[scb, m8b], writes=[sc2b])
                if debug:
                    DMA("sp", dbg["sel"][T][qc], sc2[:, qc, :], reads=[sc2b], writes=[dbgb])
                TS("dve", sc2[:, qc, :], sc2[:, qc, :], BIG, -BIG, ALU.mult, ALU.add, reads=[sc2b], writes=[sc2b])
                P.op("pe", lambda e, qc=qc: e.transpose(psum[tb][:, 0:128], sc2[:, qc, :], ident[:]),
                     reads=[sc2b, cst], writes=[psb[tb]])
                CP("act", nselT[:, qc * 128:(qc + 1) * 128], psum[tb][:, 0:128], reads=[psb[tb]], writes=[nsb])
            for hh in range(half_heads):
                OS, DS, OW, DW = 2, 3, 4, 5
                nsl = 4 * T + 4
                for sc in range(nsl):
                    sbk = sc % 2
                    MM(psum[sbk][:, :], KsT[:, sc * 128:(sc + 1) * 128], qt[:, hh, :], True, False,
                       reads=[rwb, qtb], writes=[psb[sbk]], inc=False)
                    MM(psum[sbk][:, :], ones3[:], rowb[:, hh, :], False, False, reads=[cst, rwb], writes=[psb[sbk]], inc=False)
                    MM(psum[sbk][:, :], Et[:, sc, :], nselT[:], False, True, reads=[rwb, nsb], writes=[psb[sbk]])
                    pt, ptb = pring.next()
                    mi = sc - 4 * T + M0
                    ACTF(pt[:], psum[sbk][:, :], AF.Exp, scale=SCL, bias=AB[:, hh, mi:mi + 1], reads=[psb[sbk], cst], writes=[ptb])
                    if sc >= 4 * T:
                        mask_causal(pt, ptb, t0, sc)
                    MM(psum[OS][:, :], Vsl[:, sc, :], pt[:], sc == 0, sc == nsl - 1, reads=[rwb, ptb], writes=[psb[OS]], inc=False)
                    MM(psum[DS][:, :], onesb16[:], pt[:], sc == 0, sc == nsl - 1, reads=[cst, ptb], writes=[psb[DS]])
                wl = [sc for sc in range(4 * T - 4, 4 * T + 4) if sc >= 0]
                for wi, sc in enumerate(wl):
                    sbk = wi % 2
                    MM(psum[sbk][:, :], KwT[:, sc * 128:(sc + 1) * 128], qt[:, hh, :], True, False,
                       reads=[rwb, qtb], writes=[psb[sbk]], inc=False)
                    MM(psum[sbk][:, :], ones3[:], rowb[:, hh, :], False, True, reads=[cst, rwb], writes=[psb[sbk]])
                    pt, ptb = pring.next()
                    mi = sc - 4 * T + M0
                    ACTF(pt[:], psum[sbk][:, :], AF.Exp, scale=SCL, bias=AB[:, hh, mi:mi + 1], reads=[psb[sbk], cst], writes=[ptb])
                    if sc >= 4 * T:
                        mask_causal(pt, ptb, t0, sc)
                    else:
                        mask_window(pt, ptb, t0, sc)
                    MM(psum[OW][:, :], Vwn[:, sc, :], pt[:], wi == 0, wi == len(wl) - 1, reads=[rwb, ptb], writes=[psb[OW]], inc=False)
                    MM(psum[DW][:, :], onesb16[:], pt[:], wi == 0, wi == len(wl) - 1, reads=[cst, ptb], writes=[psb[DW]])
                f1, f1b = gate_factor(gtile, gtlb, hh, 1, DS)
                TTo("dve", oacc[:], psum[OS][:, :], f1[:], ALU.mult, reads=[psb[OS], f1b], writes=[oaccb])
                f2, f2b = gate_factor(gtile, gtlb, hh, 2, DW)
                tf, tfb = tmpf.next()
                TTo("dve", tf[:], psum[OW][:, :], f2[:], ALU.mult, reads=[psb[OW], f2b], writes=[tfb])
                TTo("pool", oacc[:], oacc[:], tf[:], ALU.add, reads=[tfb, oaccb], writes=[oaccb])
                ot, otb = oout.next()
                TTo("pool", ot[:], oacc[:], ocs[:, hh, :], ALU.add, reads=[oaccb, ocb], writes=[otb])
                DMA("sp", oT[hh][:, t0:t0 + 512], ot[:], reads=[otb], writes=[ob])
        P.finish([ob] + ([dbgb] if debug else []))
        P.emit(st)
    return nc


D_MODEL, SEQ, D_FF, PLE_DIM, DEPTH = 4096, 8192, 10944, 256, 4
NCORES, TPC = 8, 1024
_KC = D_MODEL // 128
_PROGS = {}


def _prog(key, fn):
    if key not in _PROGS:
        _PROGS[key] = fn()
    return _PROGS[key]


def _run(nc, in_maps):
    res = run_bass_kernel_spmd(nc, in_maps, core_ids=list(range(NCORES)))
    return res.results


def _with_halo(aT, dtype):
    outs = []
    for c in range(NCORES):
        t0 = c * TPC
        a = np.zeros((D_MODEL, TPC + 2), dtype)
        if c > 0:
            a[:, 0:2] = aT[:, t0 - 2:t0]
        a[:, 2:] = aT[:, t0:t0 + TPC]
        outs.append(a.reshape(_KC, 128, TPC + 2))
    return outs


def _gvec(vs):
    g = np.zeros((6, D_MODEL), np.float32)
    for i, v in enumerate(vs):
        if v is not None:
            g[i] = v
    return np.ascontiguousarray(g.reshape(6, _KC, 128).transpose(2, 0, 1))


def kernel(x, p, mix_pre_norm, mix_post_norm, ffn_pre_norm, ffn_post_norm,
           ml_w_in, ml_b_if, ml_head_norm, ml_w_out,
           nsa_w_in, nsa_cmp_pe, nsa_cmp_w1, nsa_cmp_w2, nsa_w_out,
           ffn_w_gate_up, ffn_conv_w, ffn_conv_b, ffn_w_down,
           ple_w_proj, ple_norm, ple_w_gate):
    f32 = np.float32
    bf16 = ml_dtypes.bfloat16
    A = lambda a: np.asarray(a)
    x = A(x); p = A(p)
    xT = np.ascontiguousarray(x[0].T.astype(f32))
    nc_pre = _prog("pre", lambda: build_dense(D_MODEL, D_FF, PLE_DIM, mode="pre"))
    xh = _with_halo(xT, f32)
    gv = _gvec([None, None, None, None, A(mix_pre_norm)[0], None])
    res = _run(nc_pre, [dict(xT=xh[c], gvec=gv) for c in range(NCORES)])
    hT_all = np.concatenate([res[c]["hTn"] for c in range(NCORES)], axis=2)
    selB = np.zeros((2, 128), f32); selB[1] = 1
    c12 = np.array([[1, 0], [0, -1]], f32)
    NFC = (D_FF + 127) // 128
    for i in range(DEPTH):
        j = i // 2
        oT_full = np.zeros((D_MODEL, SEQ), bf16)
        if i % 2 == 0:
            nc_m = _prog("mlstm", lambda: build_mlstm(D_MODEL, SEQ, 256, 512))
            w = A(ml_w_in)[j]; b = A(ml_b_if)[j]; hnv = A(ml_head_norm)[j]
            maps = []
            for hd in range(NCORES):
                maps.append(dict(
                    hT=hT_all,
                    wq=np.ascontiguousarray(w[:, hd * 256:(hd + 1) * 256]),
                    wk=np.ascontiguousarray(w[:, 2048 + hd * 256:2048 + (hd + 1) * 256]),
                    wv=np.ascontiguousarray(w[:, 4096 + hd * 512:4096 + (hd + 1) * 512]),
                    wo=np.ascontiguousarray(w[:, 8192 + hd * 512:8192 + (hd + 1) * 512]),
                    wif=np.ascontiguousarray(np.stack([w[:, 12288 + hd], w[:, 12296 + hd]], axis=1)),
                    bif=np.array([[b[hd]], [b[8 + hd]]], f32),
                    hn=np.ascontiguousarray(hnv[hd * 512:(hd + 1) * 512].reshape(4, 128).T),
                    selB=selB, c12=c12))
            res = _run(nc_m, maps)
            for hd in range(NCORES):
                oT_full[hd * 512:(hd + 1) * 512] = res[hd]["oT"].reshape(512, SEQ)
            w_out = A(ml_w_out)[j]
        else:
            nc_n = _prog("nsa", lambda: build_nsa(D_MODEL, SEQ, 4))
            w = A(nsa_w_in)[j]
            maps = []
            perms = []
            for c in range(NCORES):
                g, half = c // 2, c % 2
                perm = list(range(4 * half, 4 * half + 4)) + [h_ for h_ in range(8) if not (4 * half <= h_ < 4 * half + 4)]
                perms.append(perm)
                wq = w[:, g * 1024:(g + 1) * 1024].reshape(D_MODEL, 8, 128)[:, perm].reshape(D_MODEL, 1024)
                wkv = np.concatenate([w[:, 4096 + jj * 512 + g * 128:4096 + jj * 512 + (g + 1) * 128] for jj in range(6)], axis=1)
                wg = w[:, 7168 + g * 24:7168 + (g + 1) * 24].reshape(D_MODEL, 8, 3)[:, perm].reshape(D_MODEL, 24)
                cs = nsa_consts(SEQ, 32, g * 8, perm)
                maps.append(dict(hT=hT_all, wq=np.ascontiguousarray(wq), wkv=np.ascontiguousarray(wkv),
                                 wg=np.ascontiguousarray(wg), pe=A(nsa_cmp_pe)[j], w1=A(nsa_cmp_w1)[j],
                                 w2=A(nsa_cmp_w2)[j], **cs))
            res = _run(nc_n, maps)
            for c in range(NCORES):
                g = c // 2
                for hl in range(4):
                    hg = g * 8 + perms[c][hl]
                    oT_full[hg * 128:(hg + 1) * 128] = res[c]["oT"][hl]
            w_out = A(nsa_w_out)[j]
        last = (i == DEPTH - 1)
        nc_d = _prog("dense_last" if last else "dense", lambda: build_dense(D_MODEL, D_FF, PLE_DIM, mode="full", last=last))
        oh = _with_halo(oT_full, bf16)
        xh = _with_halo(xT, f32)
        gv = _gvec([A(mix_post_norm)[i], A(ffn_pre_norm)[i], A(ffn_post_norm)[i], A(ple_norm)[i],
                    None if last else A(mix_pre_norm)[i + 1], None])
        cp = np.zeros((4, NFC * 128), f32)
        cp[0:3, :D_FF] = A(ffn_conv_w)[i]
        cp[3, :D_FF] = A(ffn_conv_b)[i]
        convp = np.ascontiguousarray(cp.reshape(4, NFC, 128).transpose(2, 0, 1))
        w_gu = A(ffn_w_gate_up)[i]; w_dn = A(ffn_w_down)[i]; w_pp = A(ple_w_proj)[i]; w_pg = A(ple_w_gate)[i]
        maps = []
        for c in range(NCORES):
            pT = np.ascontiguousarray(p[i, 0, c * TPC:(c + 1) * TPC, :].T)
            maps.append(dict(xT=xh[c], gvec=gv, oT=oh[c], w_out=w_out, w_gu=w_gu, convp=convp, w_down=w_dn,
                             pT=pT, w_pp=w_pp, w_pg=w_pg))
        res = _run(nc_d, maps)
        for c in range(NCORES):
            xT[:, c * TPC:(c + 1) * TPC] = res[c]["x3T"].reshape(D_MODEL, TPC)
        if not last:
            hT_all = np.concatenate([res[c]["hTn"] for c in range(NCORES)], axis=2)
    return np.ascontiguousarray(xT.T).reshape(1, SEQ, D_MODEL).astype(f32)
```
